# Optimizing a Trainium2 kernel written in Bass

```python
import jax, jax.numpy as jnp
from jax import lax
import numpy as np

D_MODEL = 4096
BATCH = 2
SEQ = 8192
DEPTH = 1
DEC_BATCH = 8
DEC_SEQ = 16
PAST_LEN = 2048

CHUNK = 64
GLA_HEADS = 4
GLA_DK = D_MODEL // (2 * GLA_HEADS)
GLA_DV = D_MODEL // GLA_HEADS
GLA_KEY = GLA_HEADS * GLA_DK
GLA_VAL = GLA_HEADS * GLA_DV
GLA_GATE_RANK = 16
GLA_GATE_NORM = 16.0
SWA_HEADS = 64
SWA_KV_HEADS = 8
SWA_HEAD_DIM = D_MODEL // SWA_HEADS
SWA_GROUP = SWA_HEADS // SWA_KV_HEADS
WINDOW = 128
WINDOW_CHUNKS = WINDOW // CHUNK
ROPE_THETA = 10000.0
D_FF = -(-8 * D_MODEL // (3 * 256)) * 256
NORM_EPS = 1e-6
IN_SPLITS = (GLA_KEY, GLA_KEY, GLA_VAL, GLA_VAL, GLA_GATE_RANK,
             SWA_HEADS * SWA_HEAD_DIM, SWA_KV_HEADS * SWA_HEAD_DIM, SWA_KV_HEADS * SWA_HEAD_DIM,
             D_MODEL, D_MODEL)
N_IN = sum(IN_SPLITS)

kernel_name = 'gated_gla_swa_sink_streaming_step'


def rmsnorm(x, g):
    xf = x.astype(jnp.float32)
    y = xf * lax.rsqrt(jnp.mean(xf * xf, axis=-1, keepdims=True) + NORM_EPS)
    return (y * g.astype(jnp.float32)).astype(x.dtype)


def rope(x, pos):
    half = x.shape[-1] // 2
    inv = ROPE_THETA ** (-jnp.arange(half, dtype=jnp.float32) / half)
    ang = pos.astype(jnp.float32)[:, None] * inv[None, :]
    cos = jnp.cos(ang)[:, None, :]
    sin = jnp.sin(ang)[:, None, :]
    x1 = x[..., :half].astype(jnp.float32)
    x2 = x[..., half:].astype(jnp.float32)
    return jnp.concatenate([x1 * cos - x2 * sin, x2 * cos + x1 * sin], axis=-1).astype(x.dtype)


def gla_blocks(q, k, v, log_a, s0):
    C = q.shape[2]
    b = jnp.cumsum(log_a, axis=2)
    b_mid = b[:, :, C // 2:C // 2 + 1]
    b_last = b[:, :, -1]
    scores = jnp.einsum('bnihd,bnjhd->bnhij', q * jnp.exp(b - b_mid), k * jnp.exp(b_mid - b))
    scores = jnp.where(jnp.tril(jnp.ones((C, C), dtype=bool)), scores, 0.0)
    o_intra = jnp.einsum('bnhij,bnjhe->bnihe', scores, v)
    q_dec = q * jnp.exp(b)
    k_dec = k * jnp.exp(b_last[:, :, None] - b)

    def step(s, xs):
        qc, kc, vc, blc = xs
        o = jnp.einsum('bihd,bhde->bihe', qc, s)
        s = jnp.exp(blc)[..., None] * s + jnp.einsum('bjhd,bjhe->bhde', kc, vc)
        return s, o

    xs = (jnp.moveaxis(q_dec, 1, 0), jnp.moveaxis(k_dec, 1, 0), jnp.moveaxis(v, 1, 0), jnp.moveaxis(b_last, 1, 0))
    s_final, o_inter = lax.scan(step, s0.astype(jnp.float32), xs)
    return o_intra + jnp.moveaxis(o_inter, 0, 1), s_final


def sink_attention(q, k, v, sinks, mask):
    logits = jnp.einsum('bnqhgd,bnshd->bnhgqs', q, k).astype(jnp.float32) * (SWA_HEAD_DIM ** -0.5)
    if mask is not None:
        logits = jnp.where(mask, logits, -jnp.inf)
    sink = sinks.astype(jnp.float32).reshape(SWA_KV_HEADS, SWA_GROUP)[:, :, None, None]
    m = jnp.maximum(jnp.max(logits, axis=-1, keepdims=True), sink)
    p = jnp.exp(logits - m)
    p = p / (jnp.sum(p, axis=-1, keepdims=True) + jnp.exp(sink - m))
    return jnp.einsum('bnhgqs,bnshd->bnqhgd', p.astype(v.dtype), v)


def swa_banded(q, k, v, sinks):
    B, S = q.shape[:2]
    N = S // CHUNK
    qc = q.reshape(B, N, CHUNK, SWA_KV_HEADS, SWA_GROUP, SWA_HEAD_DIM)
    pad = ((0, 0), (WINDOW_CHUNKS, 0), (0, 0), (0, 0), (0, 0))
    kp = jnp.pad(k.reshape(B, N, CHUNK, SWA_KV_HEADS, SWA_HEAD_DIM), pad)
    vp = jnp.pad(v.reshape(B, N, CHUNK, SWA_KV_HEADS, SWA_HEAD_DIM), pad)
    kb = jnp.concatenate([kp[:, j:j + N] for j in range(WINDOW_CHUNKS + 1)], axis=2)
    vb = jnp.concatenate([vp[:, j:j + N] for j in range(WINDOW_CHUNKS + 1)], axis=2)
    key_pos = jnp.arange(N)[:, None] * CHUNK + jnp.arange((WINDOW_CHUNKS + 1) * CHUNK)[None, :] - WINDOW
    mask = (key_pos >= 0)[None, :, None, None, None, :]
    return sink_attention(qc, kb, vb, sinks, mask).reshape(B, S, D_MODEL)


def swa_cached(q, k_new, v_new, k_past, v_past, sinks):
    B, S = q.shape[:2]
    qc = q.reshape(B, 1, S, SWA_KV_HEADS, SWA_GROUP, SWA_HEAD_DIM)
    kb = jnp.concatenate([k_past, k_new.astype(k_past.dtype)], axis=1)[:, None]
    vb = jnp.concatenate([v_past, v_new.astype(v_past.dtype)], axis=1)[:, None]
    return sink_attention(qc, kb, vb, sinks, None).reshape(B, S, D_MODEL)


def trunk_layer(x, c, pos, s0, k_past, v_past, w_ada, b_ada, g_norm1, w_in, w_gk_up, b_gk,
                g_gla_out, swa_sinks, w_out, g_norm2, w_gate_up, w_down):
    dt = x.dtype
    B, S = x.shape[:2]
    mod = jnp.einsum('bd,de->be', jax.nn.silu(c), w_ada) + b_ada
    sh1, sc1, gt1, sh2, sc2, gt2 = [m[:, None, :] for m in jnp.split(mod, 6, axis=-1)]
    h = rmsnorm(x, g_norm1) * (1.0 + sc1) + sh1
    proj = jnp.einsum('bsd,de->bse', h, w_in)
    split_pts = [int(i) for i in np.cumsum(IN_SPLITS)[:-1]]
    qa, ka, va, ga, ra, qb, kb, vb, za, zb = jnp.split(proj, split_pts, axis=-1)

    log_a = jax.nn.log_sigmoid((jnp.einsum('bsr,rk->bsk', ra, w_gk_up) + b_gk).astype(jnp.float32)) / GLA_GATE_NORM
    blk_len = min(CHUNK, S)
    n_blk = S // blk_len

    def blk(t, dh):
        return t.reshape(B, n_blk, blk_len, GLA_HEADS, dh).astype(jnp.float32)

    o_a, s_new = gla_blocks(blk(qa * (GLA_DK ** -0.5), GLA_DK), blk(ka, GLA_DK), blk(va, GLA_DV),
                            blk(log_a, GLA_DK), s0)
    o_a = rmsnorm(o_a.reshape(B, S, GLA_HEADS, GLA_DV), g_gla_out) * jax.nn.silu(ga).reshape(B, S, GLA_HEADS, GLA_DV)
    o_a = o_a.reshape(B, S, D_MODEL)

    qb = rope(qb.reshape(B, S, SWA_HEADS, SWA_HEAD_DIM), pos)
    kb = rope(kb.reshape(B, S, SWA_KV_HEADS, SWA_HEAD_DIM), pos)
    vb = vb.reshape(B, S, SWA_KV_HEADS, SWA_HEAD_DIM)
    if k_past is None:
        o_b = swa_banded(qb, kb, vb, swa_sinks)
        k_keep, v_keep = kb[:, -WINDOW:], vb[:, -WINDOW:]
    else:
        o_b = swa_cached(qb, kb, vb, k_past, v_past, swa_sinks)
        k_keep, v_keep = kb, vb

    merged = jax.nn.sigmoid(za) * o_a + jax.nn.sigmoid(zb) * o_b
    x = x + gt1 * jnp.einsum('bsd,de->bse', merged, w_out)

    h2 = rmsnorm(x, g_norm2) * (1.0 + sc2) + sh2
    gate, up = jnp.split(jnp.einsum('bsd,df->bsf', h2, w_gate_up), 2, axis=-1)
    x = x + gt2 * jnp.einsum('bsf,fd->bsd', jax.nn.silu(gate) * up, w_down)
    return x.astype(dt), s_new, k_keep, v_keep


def setup_inputs(seed: int = 0) -> dict:
    key = jax.random.key(seed)
    ks = jax.random.split(key, 24)
    f32 = jnp.float32

    def nrm(k, shape, scale):
        return jax.random.normal(k, shape, f32) * scale

    D = D_MODEL
    return {
        'x_prompt': nrm(ks[0], (BATCH, SEQ, D), 1.0),
        'x_sample': nrm(ks[1], (DEC_BATCH, DEC_SEQ, D), 1.0),
        'state_gla': nrm(ks[2], (DEPTH, DEC_BATCH, GLA_HEADS, GLA_DK, GLA_DV), 1.0),
        'cache_swa_k': nrm(ks[3], (DEPTH, DEC_BATCH, WINDOW, SWA_KV_HEADS, SWA_HEAD_DIM), 1.0),
        'cache_swa_v': nrm(ks[4], (DEPTH, DEC_BATCH, WINDOW, SWA_KV_HEADS, SWA_HEAD_DIM), 1.0),
        'c_prompt': nrm(ks[5], (BATCH, D), 1.0),
        'c_sample': nrm(ks[6], (DEC_BATCH, D), 1.0),
        'w_ada': nrm(ks[7], (DEPTH, D, 6 * D), 0.5 * D ** -0.5),
        'b_ada': nrm(ks[8], (DEPTH, 6 * D), 0.01),
        'g_norm1': 1.0 + nrm(ks[9], (DEPTH, D), 0.01),
        'w_in': nrm(ks[10], (DEPTH, D, N_IN), D ** -0.5),
        'w_gk_up': nrm(ks[11], (DEPTH, GLA_GATE_RANK, GLA_KEY), GLA_GATE_RANK ** -0.5),
        'b_gk': nrm(ks[12], (DEPTH, GLA_KEY), 0.01),
        'g_gla_out': 1.0 + nrm(ks[13], (DEPTH, GLA_DV), 0.01),
        'swa_sinks': nrm(ks[14], (DEPTH, SWA_HEADS), 1.0),
        'w_out': nrm(ks[15], (DEPTH, D, D), D ** -0.5),
        'g_norm2': 1.0 + nrm(ks[16], (DEPTH, D), 0.01),
        'w_gate_up': nrm(ks[17], (DEPTH, D, 2 * D_FF), D ** -0.5),
        'w_down': nrm(ks[18], (DEPTH, D_FF, D), D_FF ** -0.5),
        'g_final': 1.0 + nrm(ks[19], (D,), 0.01),
    }


def reference(x_prompt, x_sample, state_gla, cache_swa_k, cache_swa_v, c_prompt, c_sample,
              w_ada, b_ada, g_norm1, w_in, w_gk_up, b_gk, g_gla_out, swa_sinks, w_out,
              g_norm2, w_gate_up, w_down, g_final):
    pos_p = jnp.arange(x_prompt.shape[1])
    pos_s = PAST_LEN + jnp.arange(x_sample.shape[1])
    s0_p = jnp.zeros((x_prompt.shape[0], GLA_HEADS, GLA_DK, GLA_DV), jnp.float32)
    hp, hs = x_prompt, x_sample
    sg_p, kc_p, vc_p, sg_s, kc_s, vc_s = [], [], [], [], [], []
    for l in range(DEPTH):
        hp, s_p, k_p, v_p = trunk_layer(hp, c_prompt, pos_p, s0_p, None, None,
                                        w_ada[l], b_ada[l], g_norm1[l], w_in[l], w_gk_up[l], b_gk[l],
                                        g_gla_out[l], swa_sinks[l], w_out[l], g_norm2[l],
                                        w_gate_up[l], w_down[l])
        hs, s_s, k_s, v_s = trunk_layer(hs, c_sample, pos_s, state_gla[l], cache_swa_k[l], cache_swa_v[l],
                                        w_ada[l], b_ada[l], g_norm1[l], w_in[l], w_gk_up[l], b_gk[l],
                                        g_gla_out[l], swa_sinks[l], w_out[l], g_norm2[l],
                                        w_gate_up[l], w_down[l])
        sg_p.append(s_p); kc_p.append(k_p); vc_p.append(v_p)
        sg_s.append(s_s); kc_s.append(k_s); vc_s.append(v_s)
    y_prompt = rmsnorm(hp, g_final)
    y_sample = rmsnorm(hs, g_final)
    return (y_prompt, y_sample, jnp.stack(sg_p), jnp.stack(kc_p), jnp.stack(vc_p),
            jnp.stack(sg_s), jnp.stack(kc_s), jnp.stack(vc_s))
```

```python
import contextlib
import os
KSTOP = float(os.environ.get('KSTOP', '99'))
import numpy as np
import concourse.bass as bass
import concourse.mybir as mybir
from concourse.bass_utils import run_bass_kernel_spmd

F32 = mybir.dt.float32
BF16 = mybir.dt.bfloat16
AF = mybir.ActivationFunctionType
ALU = mybir.AluOpType
AX = mybir.AxisListType
EPS = 1e-6
NS_DMA = 40


class Cfg:
    def __init__(self, D=4096, DFF=11008, SEQ=8192, NTOK=128, NSB=4, PAST=2048, NCORE=2):
        self.D, self.DFF, self.SEQ, self.NTOK, self.NSB, self.PAST, self.NCORE = D, DFF, SEQ, NTOK, NSB, PAST, NCORE
        self.KT = D // 128
        self.HG = 4
        self.DK = D // 8
        self.DKT = self.DK // 128
        self.DV = D // 4
        self.DVT = self.DV // 128
        self.DKEY = D // 2
        self.NQ = D // 64
        self.NKV = 8
        self.G = self.NQ // self.NKV
        self.QT = self.G // 2
        self.FT = DFF // 128
        self.DS = 16
        self.NB = 1 + NSB
        self.NBP = self.NB + (self.NB % 2)
        o = 0
        self.o_qa = o; o += self.DKEY
        self.o_ka = o; o += self.DKEY
        self.o_va = o; o += D
        self.o_ga = o; o += D
        self.o_ra = o; o += 16
        self.o_qb = o; o += D
        self.o_kb = o; o += 512
        self.o_vb = o; o += 512
        self.o_za = o; o += D
        self.o_zb = o; o += D
        self.NIN = o
        t = {}
        t[('ra',)] = [(self.o_ra, 16)]
        self.WV = min(512, self.DV); self.NBV = self.DV // self.WV
        self.WZ = self.G * 64
        it = {}
        for h in range(4):
            it[('qa', h)] = (self.o_qa + h * self.DK, self.DK)
            it[('ka', h)] = (self.o_ka + h * self.DK, self.DK)
            for i in range(self.NBV):
                it[('va', h, i)] = (self.o_va + h * self.DV + i * self.WV, self.WV)
                it[('ga', h, i)] = (self.o_ga + h * self.DV + i * self.WV, self.WV)
                it[('za', h, i)] = (self.o_za + h * self.DV + i * self.WV, self.WV)
        for g in range(8):
            it[('qb', g)] = (self.o_qb + g * self.WZ, self.WZ)
            it[('zb', g)] = (self.o_zb + g * self.WZ, self.WZ)
        it[('kb',)] = (self.o_kb, 512)
        it[('vb',)] = (self.o_vb, 512)
        self.it = it
        self.it_index = {k: i for i, k in enumerate(it.keys())}
        self.NFB = (DFF + 511) // 512
        self.in_tiles = t
        self.in_index = {k: i for i, k in enumerate(t.keys())}
        self.KG_O = (self.KT + 7) // 8
        self.KG_D = (self.FT + 7) // 8
        self.NCB = D // 512


class Prog:
    ENGS = ('pe', 'act', 'dve', 'pool', 'sp')

    def __init__(self):
        self.q = {e: [] for e in self.ENGS}
        self.n = {e: 0 for e in self.ENGS}
        self.dma_n = 0
        self.lastw = {}
        self.readers = {}
        self.seen = {e: {} for e in self.ENGS}
        self.dma_last = {}

    def op(self, eng, fn, reads=(), writes=(), dma=False):
        need = {}

        def add(tok):
            k, v = tok
            if v > need.get(k, 0):
                need[k] = v
        for t in reads:
            w = self.lastw.get(t)
            if w:
                add(w)
            if t.startswith('ps'):
                for tok in self.readers.get(t, {}).items():
                    if tok[0] != ('e', eng):
                        add(tok)
        for t in writes:
            w = self.lastw.get(t)
            if w:
                add(w)
            for tok in self.readers.get(t, {}).items():
                add(tok)
        if dma:
            slot = self.dma_n % NS_DMA
            val = 16 * (self.dma_n // NS_DMA + 1)
            self.dma_n += 1
            if val > 16:
                add((('d', slot), val - 16))
            me = (('d', slot), val)
            self.dma_last[slot] = val
        else:
            self.n[eng] += 1
            me = (('e', eng), self.n[eng])
        waits = []
        for k, v in need.items():
            if k == ('e', 'pe') and eng == 'pe':
                continue
            if v > self.seen[eng].get(k, 0):
                self.seen[eng][k] = v
                waits.append((k, v))
        self.q[eng].append((fn, waits, me, dma))
        for t in reads:
            r = self.readers.setdefault(t, {})
            if me[1] > r.get(me[0], 0):
                r[me[0]] = me[1]
        for t in writes:
            self.lastw[t] = me
            self.readers[t] = {}

    def barrier(self):
        toks = [(('e', e), self.n[e]) for e in self.ENGS if self.n[e] > 0]
        toks += [(('d', s), v) for s, v in self.dma_last.items()]
        for e in self.ENGS:
            waits = []
            for k, v in toks:
                if k == ('e', e):
                    continue
                if v > self.seen[e].get(k, 0):
                    self.seen[e][k] = v
                    waits.append((k, v))
            if waits:
                self.q[e].append((None, waits, None, False))

    def finish(self):
        waits = [(('d', s), v) for s, v in self.dma_last.items()]
        self.q['sp'].append((None, waits, None, False))

    def replay(self, eng_name, eng, sems):
        for fn, waits, me, dma in self.q[eng_name]:
            for k, v in waits:
                eng.wait_ge(sems[k], v)
            if fn is None:
                continue
            ins = fn(eng)
            ins.then_inc(sems[me[0]], 16 if dma else 1)


def build(cfg):
    c = cfg
    D, KT, NTOK, SEQ, NSB, DK, DKT, DV, DVT, FT = c.D, c.KT, c.NTOK, c.SEQ, c.NSB, c.DK, c.DKT, c.DV, c.DVT, c.FT
    NBP = c.NBP
    nc = bass.Bass("TRN2", target_bir_lowering=False)
    P = Prog()

    def din(name, shape, dt=F32):
        return nc.dram_tensor(name, list(shape), dt, kind="ExternalInput").ap()

    def dout(name, shape, dt=F32):
        return nc.dram_tensor(name, list(shape), dt, kind="ExternalOutput").ap()

    def dscr(name, shape, dt):
        return nc.dram_tensor(name, list(shape), dt, kind="Internal").ap()

    NS_TOK = NSB * c.DS
    xp = din("xp", [SEQ, D]); xs = din("xs", [NS_TOK, D])
    s0 = din("s0", [NSB * 4 * DK, DV]); ck = din("ck", [NSB * 128, 512]); cv = din("cv", [NSB * 128, 512])
    cT = din("cT", [128, KT * NBP])
    w_ada = din("w_ada", [D, 6 * D]); b_adaT = din("b_adaT", [128, 6 * KT])
    g1c = din("g1c", [128, KT]); g2c = din("g2c", [128, KT]); gfin = din("gfin", [128, D])
    w_in = din("w_in", [D, c.NIN]); w_gk = din("w_gk", [16, c.DKEY]); bgkc = din("bgkc", [128, c.DKEY // 128])
    gglac = din("gglac", [128, DVT]); sinkc = din("sinkc", [64, c.NQ])
    w_out = din("w_out", [D, D]); w_gu = din("w_gu", [D, 2 * c.DFF]); w_dn = din("w_dn", [c.DFF, D])
    cos8 = din("cos8", [SEQ + c.DS, 256]); sin8 = din("sin8", [SEQ + c.DS, 256])
    identf_d = din("identf", [128, 128]); tri_d = din("tri", [64, 64])

    y_p = dout("y_p", [SEQ, D]); y_s = dout("y_s", [NS_TOK, D])
    sg_p = dout("sg_p", [4 * DK, DV]); kc_p = dout("kc_p", [128, 512]); vc_p = dout("vc_p", [128, 512])
    sg_s = dout("sg_s", [NSB * 4 * DK, DV]); kc_s = dout("kc_s", [NS_TOK, 512]); vc_s = dout("vc_s", [NS_TOK, 512])

    NT_IN = len(c.in_tiles)
    ws_in = dscr("ws_in", [NT_IN * 128, KT * 128], BF16)
    ws_gu = dscr("ws_gu", [2 * c.NFB * c.KG_O * 128, 8 * 512], BF16)
    ws_it = dscr("ws_it", [len(c.it) * c.KG_O * 128, 8 * 512], BF16)
    ws_o = dscr("ws_o", [c.NCB * c.KG_O * 128, 8 * 512], BF16)
    ws_d = dscr("ws_d", [c.NCB * c.KG_D * 128, 8 * 512], BF16)
    sS = dscr("sS", [4 * DK, DV], F32)

    es = contextlib.ExitStack()
    with es:
        ARENA_W = 49152
        _allocs = []
        aoff = [0]

        def alloc(words):
            aoff[0] += words
            assert aoff[0] <= ARENA_W, ("SBUF overflow", aoff[0])
            t_ = es.enter_context(nc.sbuf_tensor("sb%d_%d" % (len(_allocs), words), [128, words], F32))
            _allocs.append(t_)
            return t_[:, :]

        def allocf(n):
            return alloc(n)

        def allocb(n):
            return alloc((n + 1) // 2).bitcast(BF16)

        psum = [es.enter_context(nc.psum_tensor(f"psb{i}", [128, 512], F32)) for i in range(8)]

        def PS(i):
            return psum[i][:, :]

        def PSB(i):
            return psum[i][:, :].bitcast(BF16)

        def pk(i):
            return f"ps{i}"

        WB = KT * 128 if KT * 128 >= 4096 else 4096
        WB = max(KT * 128, 8 * 512)
        NWB = 3
        wbuf = [allocb(WB) for _ in range(NWB)]
        wctr = [0]
        identf = allocf(128); identb = allocb(128); tri = allocf(64); ones = allocf(128)
        modT = allocf(6 * KT * NBP); siluT = allocf(KT * NBP); badaT = allocf(6 * KT)
        g1 = allocf(KT); g2 = allocf(KT); bgk = allocf(c.DKEY // 128); ggla = allocf(DVT); sink = allocf(c.NQ)
        wgk = allocf(128)
        s1c = allocf(KT); sh1c = allocf(KT); s2c = allocf(KT); sh2c = allocf(KT)
        gt1row = allocb(D); gt2row = allocb(D)
        hT = allocb(KT * NTOK)
        mergedT = allocb(KT * NTOK)
        xsb = allocb(D)
        NSUB = (NTOK + 127) // 128
        x1 = [allocf(D) for _ in range(NSUB)]
        NCHM_ = NTOK // 64
        stat = allocf(64)
        raT = allocf(NTOK)
        tmpA = allocf(NTOK); tmpB = allocf(NTOK)
        R_WORDS = max(4096, (FT * NTOK + 1) // 2, 8 * DKT * NTOK + (NCHM_ * (DK + DV) + DVT * NTOK) // 2 + NTOK + DVT * NTOK // 2 + DKT * DV * 3 // 2 + 16)
        Rb = alloc(R_WORDS)
        roff = [0]
        def ralloc(words):
            o = roff[0]; roff[0] += words
            assert roff[0] <= R_WORDS, ('R overflow', roff[0], R_WORDS)
            return Rb[:, o:o + words]
        allocf_save, allocb_save = allocf, allocb
        allocf = lambda n: ralloc(n)
        allocb = lambda n: ralloc((n + 1) // 2).bitcast(BF16)
        bT = allocf(DKT * NTOK); nbT = allocf(DKT * NTOK)
        E1 = allocf(DKT * NTOK); E2 = allocf(DKT * NTOK); E3 = allocf(DKT * NTOK); E4 = allocf(DKT * NTOK)
        qpT = allocb(DKT * NTOK); qdT = allocb(DKT * NTOK); kpT = allocb(DKT * NTOK); kdT = allocb(DKT * NTOK)
        NCHM = NCHM_
        kd = allocb(NCHM * DK); vh = allocb(NCHM * DV); vTh = allocb(DVT * NTOK)
        gtmp = allocb(NTOK); ztmp = allocb(NTOK); gate = allocb(DVT * NTOK)
        Sf = allocf(DKT * DV); Sb = allocb(DKT * DV)
        allocf, allocb = allocf_save, allocb_save
        sc = allocb(64); otok = allocb(DV)
        KW = 128 + NTOK
        kz = allocb(8 * 2 * KW)
        Vbuf = allocb((2 + NCHM) * 512)
        assert NTOK <= 128
        gtok = allocb(512); ztok = allocb(512); rtok = allocb(512); kdup = allocb(1024)
        cos8t = allocf(256); sin8t = allocf(256); rta = allocf(256); rtb = allocf(256)
        cdupb = allocb(128)
        qT = allocb(c.QT * NTOK); sigzb = allocb(c.QT * NTOK)
        pexp = allocb(8 * 256); pT = allocb(3 * 8 * 64); ob = allocb(512)
        cstage = allocf(512); cdup = allocf(128)
        kvout = cstage; kvout2 = cstage
        mtmp = allocb(c.QT * 64)
        actT = Rb[:, :(FT * NTOK + 1) // 2].bitcast(BF16)
        print("SBUF words used", aoff[0], "of", ARENA_W)

        def v3(ap, a, b):
            return ap.rearrange("p (a b) -> p a b", a=a, b=b)

        hT3 = v3(hT, KT, NTOK); mT3 = v3(mergedT, KT, NTOK)
        kz4 = kz.rearrange("p (g h k) -> p g h k", g=8, h=2, k=KW)
        Vb3 = v3(Vbuf, 2 + NCHM, 512)

        def dma(out, in_, reads=(), writes=(), nonc=False):
            def fn(e, out=out, in_=in_):
                if nonc:
                    return e.dma_start(out=out, in_=in_, allow_slow_non_contiguous=True)
                return e.dma_start(out=out, in_=in_)
            P.op('sp', fn, reads, writes, dma=True)

        def act(out, in_, func, reads, writes, bias=None, scale=None, accum=None):
            def fn(e):
                kw = {}
                if bias is not None:
                    kw['bias'] = bias
                if scale is not None:
                    kw['scale'] = scale
                if accum is not None:
                    kw['accum_out'] = accum
                return e.activation(out=out, in_=in_, func=func, **kw)
            P.op('act', fn, reads, writes)

        def dve(kind, reads, writes, eng='dve', **kw):
            def fn(e):
                return getattr(e, kind)(**kw)
            P.op(eng, fn, reads, writes)

        def pe_group(fns, reads, writes):
            def fn(e):
                ins = None
                for f in fns:
                    ins = f(e)
                return ins
            P.op('pe', fn, reads, writes)

        def mm(out, lhsT, rhs, start, stop):
            return lambda e: e.matmul(out, lhsT=lhsT, rhs=rhs, start=start, stop=stop)

        def tr(out, in_, ident):
            return lambda e: e.transpose(out=out, in_=in_, identity=ident)

        dma(identf, identf_d[:, :], (), ['identf'])
        dma(tri[:64, :], tri_d[:, :], (), ['tri'])
        dve('tensor_copy', ['identf'], ['identb'], out=identb, in_=identf)
        dve('memset', (), ['ones'], ap=ones, constant=1.0)
        dma(siluT, cT[:, :], (), ['siluT'])
        dma(badaT, b_adaT[:, :], (), ['badaT'])
        dma(g1, g1c[:, :], (), ['g1']); dma(g2, g2c[:, :], (), ['g2'])
        dma(bgk, bgkc[:, :], (), ['bgk']); dma(ggla, gglac[:, :], (), ['ggla'])
        dma(sink[:64, :], sinkc[:, :], (), ['sink'])
        act(siluT, siluT, AF.Silu, ['siluT'], ['siluT'])

        assert D >= 4096 or True
        cvf = [x1[0], Rb[:, :4096]] if D >= 4096 else [allocf(4096), Rb[:, :4096]]
        cvb = wbuf
        cctr = [0]

        def convert_tile(src_aps, dst_ap, nel):
            i = cctr[0]; cctr[0] += 1
            fb = cvf[i % 2]; bb = cvb[i % NWB]
            for (dst_view, src) in src_aps(fb):
                dma(dst_view, src, (), [f'cvf{i % 2}'])
            eng = 'dve' if i % 2 == 0 else 'pool'
            dve('tensor_copy', [f'cvf{i % 2}'], [f'wbuf{i % NWB}'], eng=eng, out=bb[:, :nel], in_=fb[:, :nel])
            dma(dst_ap, bb[:, :nel], [f'wbuf{i % NWB}'], [])

        def fm_src(W, runs):
            def f(fb):
                v = fb[:, :KT * 128].rearrange("p (k c) -> p k c", k=KT, c=128)
                out = []
                o = 0
                for (c0, n) in runs:
                    for k0 in range(0, KT, 8):
                        k1 = min(KT, k0 + 8)
                        out.append((v[:, k0:k1, o:o + n], W[k0 * 128:k1 * 128, c0:c0 + n].rearrange("(k p) c -> p k c", p=128)))
                    o += n
                return out
            return f

        for key, runs in c.in_tiles.items():
            idx = c.in_index[key]
            ncols = sum(n for _, n in runs)
            if ncols < 128:
                pass
            convert_tile(fm_src(w_in, runs), ws_in[idx * 128:(idx + 1) * 128, :], KT * 128)
        def tm_src(W, kt0, nk, c0, w=512):
            def f(fb):
                v = fb[:, :8 * 512].rearrange("p (k c) -> p k c", k=8, c=512)
                return [(v[:, :nk, :w], W[kt0 * 128:(kt0 + nk) * 128, c0:c0 + w].rearrange("(k p) c -> p k c", p=128))]
            return f

        for key, (c0, w) in c.it.items():
            idx = c.it_index[key]
            for kg in range(c.KG_O):
                nk = min(8, KT - kg * 8)
                i = idx * c.KG_O + kg
                convert_tile(tm_src(w_in, kg * 8, nk, c0, w), ws_it[i * 128:(i + 1) * 128, :], 8 * 512)
        for fb_ in range(c.NFB):
            w = min(512, c.DFF - fb_ * 512)
            for part in range(2):
                for kg in range(c.KG_O):
                    nk = min(8, KT - kg * 8)
                    i = (part * c.NFB + fb_) * c.KG_O + kg
                    convert_tile(tm_src(w_gu, kg * 8, nk, part * c.DFF + fb_ * 512, w), ws_gu[i * 128:(i + 1) * 128, :], 8 * 512)

        for cb in range(c.NCB):
            for kg in range(c.KG_O):
                nk = min(8, KT - kg * 8)
                i = cb * c.KG_O + kg
                convert_tile(tm_src(w_out, kg * 8, nk, cb * 512), ws_o[i * 128:(i + 1) * 128, :], 8 * 512)
            for kg in range(c.KG_D):
                nk = min(8, FT - kg * 8)
                i = cb * c.KG_D + kg
                convert_tile(tm_src(w_dn, kg * 8, nk, cb * 512), ws_d[i * 128:(i + 1) * 128, :], 8 * 512)

        modT3 = v3(modT, 6 * KT, NBP); siluT3 = v3(siluT, KT, NBP)
        for m in range(6 * KT):
            i = cctr[0]; cctr[0] += 1
            fb = cvf[i % 2]
            v = fb[:, :KT * 128].rearrange("p (k c) -> p k c", k=KT, c=128)
            for k0 in range(0, KT, 8):
                k1 = min(KT, k0 + 8)
                dma(v[:, k0:k1, :], w_ada[k0 * 128:k1 * 128, m * 128:(m + 1) * 128].rearrange("(k p) c -> p k c", p=128), (), [f'cvf{i % 2}'])
            bank = m % 4
            pe_group([mm(PS(bank)[:, :NBP], v[:, kt, :], siluT3[:, kt, :], kt == 0, kt == KT - 1) for kt in range(KT)],
                     [f'cvf{i % 2}', 'siluT'], [pk(bank)])
            act(modT3[:, m, :], PS(bank)[:, :NBP], AF.Identity, [pk(bank), 'badaT'], ['modT'], bias=badaT[:, m:m + 1])
        P.barrier()

        def load_mod(b):
            dve('scalar_tensor_tensor', ['modT', 'g1'], ['s1c'], out=s1c, in0=modT3[:, KT:2 * KT, b], scalar=1.0, in1=g1,
                op0=ALU.add, op1=ALU.mult)
            dve('tensor_copy', ['modT'], ['sh1c'], out=sh1c, in_=modT3[:, 0:KT, b])
            dve('scalar_tensor_tensor', ['modT', 'g2'], ['s2c'], out=s2c, in0=modT3[:, 4 * KT:5 * KT, b], scalar=1.0, in1=g2,
                op0=ALU.add, op1=ALU.mult)
            dve('tensor_copy', ['modT'], ['sh2c'], out=sh2c, in_=modT3[:, 3 * KT:4 * KT, b])
            for (chunk, row, nm) in ((2, gt1row, 'gt1row'), (5, gt2row, 'gt2row')):
                for kt in range(KT):
                    bank = 4 + (kt % 2)
                    dve('tensor_scalar', ['ones', 'modT'], ['tmpA'], out=tmpA[:, :128], in0=_ones128,
                        scalar1=modT3[:, chunk * KT + kt, b:b + 1], scalar2=None, op0=ALU.mult)
                    pe_group([mm(PS(bank)[:, :128], tmpA[:, :128], identf, True, True)], ['tmpA', 'identf'], [pk(bank)])
                    act(row[:, kt * 128:(kt + 1) * 128], PS(bank)[:, :128], AF.Copy, [pk(bank)], [nm])

        _ones128 = allocf(128)
        dve('memset', (), ['ones'], ap=_ones128, constant=1.0)

        def wload(src_rows, nel):
            i = wctr[0] % NWB; wctr[0] += 1
            dma(wbuf[i][:, :nel], src_rows, [], [f'wbuf{i}'])
            return i

        lin_bank = [0]

        def fm_linear(tiles, ntok, epilogue, src=None, kt_n=None, xT=None, xkey='hT'):
            xT3 = hT3 if xT is None else xT
            banks = []
            for (rows, ncols) in tiles:
                wi = wload(rows, KT * 128)
                w3 = wbuf[wi][:, :KT * 128].rearrange("p (k c) -> p k c", k=KT, c=128)
                bank = lin_bank[0] % 4; lin_bank[0] += 1
                pe_group([mm(PS(bank)[:ncols, :ntok], w3[:, kt, :ncols], xT3[:, kt, :ntok], kt == 0, kt == KT - 1)
                          for kt in range(KT)], [f'wbuf{wi}', xkey], [pk(bank)])
                banks.append(bank)
            epilogue(banks)

        def in_tile(key):
            idx = c.in_index[key]
            return (ws_in[idx * 128:(idx + 1) * 128, :], sum(n for _, n in c.in_tiles[key]))

        tm_set = [0]

        def tm_linear(ws, blocks, KG, KTOT, aT3, akey, ntok, epilogue, sub=128):
            nsub = (ntok + sub - 1) // sub
            for bi, (blk, w) in enumerate(blocks):
                base = 4 * (tm_set[0] % 2); tm_set[0] += 1
                for kg in range(KG):
                    nk = min(8, KTOT - kg * 8)
                    i = blk * KG + kg
                    wi = wload(ws[i * 128:(i + 1) * 128, :], 8 * 512)
                    w3 = wbuf[wi][:, :8 * 512].rearrange("p (k c) -> p k c", k=8, c=512)
                    for s in range(nsub):
                        nr = min(sub, ntok - s * sub)
                        pe_group([mm(PS(base + s)[:nr, :w], aT3[:, kg * 8 + k, s * sub:s * sub + nr], w3[:, k, :w],
                                     kg == 0 and k == 0, kg == KG - 1 and k == nk - 1) for k in range(nk)],
                                 [f'wbuf{wi}', akey], [pk(base + s)])
                for s in range(nsub):
                    nr = min(sub, ntok - s * sub)
                    epilogue(bi, s, nr, base + s, w)

        trb = [0]

        def trbank():
            trb[0] += 1
            return 6 + (trb[0] % 2)

        def fm_from_tok(tok, tokkey, nr, ntile, bank):
            pe_group([tr(PSB(bank)[:, j * 128:j * 128 + nr], tok[:nr, j * 128:(j + 1) * 128], identb[:nr, :nr]) for j in range(ntile)],
                     [tokkey, 'identb'], [pk(bank)])

        def itb(key):
            return (c.it_index[key], c.it[key][1])

        def rope(bank, nr, nh, dst):
            x = PS(bank)[:nr, :nh * 64].rearrange("p (h t i) -> p h t i", h=nh, t=2, i=32)
            d = dst[:nr, :nh * 64].rearrange("p (h t i) -> p h t i", h=nh, t=2, i=32)
            cs = cos8t[:nr, :nh * 32].rearrange("p (h i) -> p h i", h=nh, i=32)
            sn = sin8t[:nr, :nh * 32].rearrange("p (h i) -> p h i", h=nh, i=32)
            ta = rta[:nr, :nh * 32].rearrange("p (h i) -> p h i", h=nh, i=32)
            tb = rtb[:nr, :nh * 32].rearrange("p (h i) -> p h i", h=nh, i=32)
            dk = getattr(dst, '_key', None)
            dve('tensor_tensor', [pk(bank), 'cos8t'], ['rta'], out=ta, in0=x[:, :, 0, :], in1=cs, op=ALU.mult)
            dve('tensor_tensor', [pk(bank), 'sin8t'], ['rtb'], out=tb, in0=x[:, :, 1, :], in1=sn, op=ALU.mult)
            dve('tensor_tensor', ['rta', 'rtb'], [rope_key[0]], eng='pool', out=d[:, :, 0, :], in0=ta, in1=tb, op=ALU.subtract)
            dve('tensor_tensor', [pk(bank), 'cos8t'], ['rta'], out=ta, in0=x[:, :, 1, :], in1=cs, op=ALU.mult)
            dve('tensor_tensor', [pk(bank), 'sin8t'], ['rtb'], out=tb, in0=x[:, :, 0, :], in1=sn, op=ALU.mult)
            dve('tensor_tensor', ['rta', 'rtb'], [rope_key[0]], eng='pool', out=d[:, :, 1, :], in0=ta, in1=tb, op=ALU.add)

        rope_key = ['rtok']

        def rstd_from_ss(ss_ap, n, dim):
            dve('tensor_scalar', ['stat'], ['stat'], out=ss_ap, in0=ss_ap, scalar1=1.0 / dim, scalar2=EPS, op0=ALU.mult, op1=ALU.add)
            act(ss_ap, ss_ap, AF.Sqrt, ['stat'], ['stat'])
            dve('reciprocal', ['stat'], ['stat'], out=ss_ap, in_=ss_ap)

        def norm_to_T(src_tile, src_key, s, nr, scol, shcol, skeys):
            ss = stat[:nr, 0:1]
            act(xsb[:nr, :], src_tile[:nr, :], AF.Square, [src_key], ['xsb', 'stat'], accum=ss)
            rstd_from_ss(ss, nr, D)
            dve('tensor_scalar', [src_key, 'stat'], ['xsb'], out=xsb[:nr, :], in0=src_tile[:nr, :], scalar1=ss, scalar2=None, op0=ALU.mult)
            for k0 in range(0, KT, 4):
                bank = 4 + ((k0 // 4) % 2)
                kk = min(4, KT - k0)
                pe_group([tr(PSB(bank)[:, j * 128:j * 128 + nr], xsb[:nr, (k0 + j) * 128:(k0 + j + 1) * 128], identb[:nr, :nr])
                          for j in range(kk)], ['xsb', 'identb'], [pk(bank)])
                for j in range(kk):
                    act(hT3[:, k0 + j, s * 128:s * 128 + nr], PSB(bank)[:, j * 128:j * 128 + nr], AF.Identity,
                        [pk(bank)] + skeys, ['hT'], bias=shcol[:, k0 + j:k0 + j + 1], scale=scol[:, k0 + j:k0 + j + 1])

        def process_tile(x_src, y_dst, ntok, C, pos0, first, S_in, S_out, hist, kv_out):
            nch = ntok // C
            nsub = (ntok + 127) // 128
            half = C // 2
            if KSTOP < 1:
                return
            for s in range(nsub):
                nr = min(128, ntok - s * 128)
                dma(x1[s][:nr, :], x_src[s * 128:s * 128 + nr, :], [], ['x1_%d' % s])
                norm_to_T(x1[s], 'x1_%d' % s, s, nr, s1c, sh1c, ['s1c', 'sh1c'])
            dma(cos8t[:ntok, :], cos8[pos0:pos0 + ntok, :], [], ['cos8t'])
            dma(sin8t[:ntok, :], sin8[pos0:pos0 + ntok, :], [], ['sin8t'])
            if KSTOP < 2:
                return
            def ep_ra(banks):
                act(raT[:16, :ntok], PS(banks[0])[:16, :ntok], AF.Copy, [pk(banks[0])], ['raT'])
            fm_linear([in_tile(('ra',))], ntok, ep_ra)
            bT3 = v3(bT, DKT, NTOK); nbT3 = v3(nbT, DKT, NTOK)
            E13, E23, E33, E43 = (v3(e, DKT, NTOK) for e in (E1, E2, E3, E4))
            qpT3, qdT3, kpT3, kdT3 = (v3(e, DKT, NTOK) for e in (qpT, qdT, kpT, kdT))
            kd3 = v3(kd, NCHM, DK); vh3 = v3(vh, NCHM, DV); vTh3 = v3(vTh, DVT, NTOK); gate3 = v3(gate, DVT, NTOK)
            Sf3 = v3(Sf, DKT, DV); Sb3 = v3(Sb, DKT, DV)
            for h in range(4):
                for dt in range(DKT):
                    gd = h * DKT + dt
                    bank = lin_bank[0] % 4; lin_bank[0] += 1
                    dma(wgk[:16, :128], w_gk[:, gd * 128:(gd + 1) * 128], [], ['wgk'])
                    pe_group([mm(PS(bank)[:, :ntok], wgk[:16, :128], raT[:16, :ntok], True, True)],
                             ['wgk', 'raT'], [pk(bank)])
                    dve('tensor_scalar', [pk(bank), 'bgk'], ['tmpA'], out=tmpA[:, :ntok], in0=PS(bank)[:, :ntok],
                        scalar1=bgk[:, gd:gd + 1], scalar2=None, op0=ALU.add)
                    act(tmpB[:, :ntok], tmpA[:, :ntok], AF.Abs, ['tmpA'], ['tmpB'])
                    act(tmpB[:, :ntok], tmpB[:, :ntok], AF.Exp, ['tmpB'], ['tmpB'], scale=-1.0)
                    act(tmpB[:, :ntok], tmpB[:, :ntok], AF.Ln, ['tmpB'], ['tmpB'], bias=1.0)
                    dve('scalar_tensor_tensor', ['tmpA', 'tmpB'], ['tmpA'], out=tmpA[:, :ntok], in0=tmpA[:, :ntok], scalar=0.0,
                        in1=tmpB[:, :ntok], op0=ALU.min, op1=ALU.subtract)
                    act(tmpA[:, :ntok], tmpA[:, :ntok], AF.Copy, ['tmpA'], ['tmpA'], scale=1.0 / 16.0)
                    for ch in range(nch):
                        dve('tensor_tensor_scan', ['tmpA', 'ones'], ['bT'], out=bT3[:, dt, ch * C:(ch + 1) * C], data0=ones[:, :C],
                            data1=tmpA[:, ch * C:(ch + 1) * C], initial=0.0, op0=ALU.mult, op1=ALU.add)
                act(nbT3[:, :, :ntok], bT3[:, :, :ntok], AF.Copy, ['bT'], ['nbT'], scale=-1.0)
                act(E33[:, :, :ntok], bT3[:, :, :ntok], AF.Exp, ['bT'], ['E3'])
                for dt in range(DKT):
                    for ch in range(nch):
                        sl = slice(ch * C, (ch + 1) * C)
                        mid = ch * C + half; last = ch * C + C - 1
                        act(E13[:, dt, sl], bT3[:, dt, sl], AF.Exp, ['bT', 'nbT'], ['E1'], bias=nbT3[:, dt, mid:mid + 1])
                        act(E23[:, dt, sl], nbT3[:, dt, sl], AF.Exp, ['bT', 'nbT'], ['E2'], bias=bT3[:, dt, mid:mid + 1])
                        act(E43[:, dt, sl], nbT3[:, dt, sl], AF.Exp, ['bT', 'nbT'], ['E4'], bias=bT3[:, dt, last:last + 1])
                sck = DK ** -0.5

                def ep_qa(bi, s_, nr, bank, w):
                    dve('tensor_copy', [pk(bank)], ['gtok'], out=gtok[:nr, :w], in_=PS(bank)[:nr, :w])
                    tb = trbank()
                    fm_from_tok(gtok, 'gtok', nr, DKT, tb)
                    for dt in range(DKT):
                        dve('scalar_tensor_tensor', [pk(tb), 'E1'], ['qpT'], out=qpT3[:, dt, :nr], in0=PSB(tb)[:, dt * 128:dt * 128 + nr],
                            scalar=sck, in1=E13[:, dt, :nr], op0=ALU.mult, op1=ALU.mult)
                        dve('scalar_tensor_tensor', [pk(tb), 'E3'], ['qdT'], out=qdT3[:, dt, :nr], in0=PSB(tb)[:, dt * 128:dt * 128 + nr],
                            scalar=sck, in1=E33[:, dt, :nr], op0=ALU.mult, op1=ALU.mult)
                tm_linear(ws_it, [itb(('qa', h))], c.KG_O, KT, hT3, 'hT', ntok, ep_qa)

                def ep_ka(bi, s_, nr, bank, w):
                    act(ztok[:nr, :w], PS(bank)[:nr, :w], AF.Copy, [pk(bank)], ['ztok'])
                    tb = trbank()
                    fm_from_tok(ztok, 'ztok', nr, DKT, tb)
                    for dt in range(DKT):
                        dve('tensor_tensor', [pk(tb), 'E2'], ['kpT'], out=kpT3[:, dt, :nr], in0=PSB(tb)[:, dt * 128:dt * 128 + nr],
                            in1=E23[:, dt, :nr], op=ALU.mult)
                        dve('tensor_tensor', [pk(tb), 'E4'], ['kdT'], out=kdT3[:, dt, :nr], in0=PSB(tb)[:, dt * 128:dt * 128 + nr],
                            in1=E43[:, dt, :nr], op=ALU.mult)
                tm_linear(ws_it, [itb(('ka', h))], c.KG_O, KT, hT3, 'hT', ntok, ep_ka)

                def ep_va(bi, s_, nr, bank, w):
                    act(vh3[:nr, s_, bi * c.WV:bi * c.WV + w], PS(bank)[:nr, :w], AF.Copy, [pk(bank)], ['vh'])
                tm_linear(ws_it, [itb(('va', h, i)) for i in range(c.NBV)], c.KG_O, KT, hT3, 'hT', ntok, ep_va, sub=C)

                for i in range(c.NBV):
                    def ep_ga(bi, s_, nr, bank, w):
                        act(gtok[:nr, :w], PS(bank)[:nr, :w], AF.Silu, [pk(bank)], ['gtok'])
                    tm_linear(ws_it, [itb(('ga', h, i))], c.KG_O, KT, hT3, 'hT', ntok, ep_ga)

                    def ep_za(bi, s_, nr, bank, w, i=i):
                        act(ztok[:nr, :w], PS(bank)[:nr, :w], AF.Sigmoid, [pk(bank)], ['ztok'])
                        dve('tensor_tensor', ['gtok', 'ztok'], ['gtok'], eng='pool', out=gtok[:nr, :w], in0=gtok[:nr, :w], in1=ztok[:nr, :w], op=ALU.mult)
                        tb = trbank()
                        nt = w // 128
                        fm_from_tok(gtok, 'gtok', nr, nt, tb)
                        for j in range(nt):
                            jv = i * (c.WV // 128) + j
                            act(gate3[:, jv, s_ * 128:s_ * 128 + nr], PSB(tb)[:, j * 128:j * 128 + nr], AF.Identity, [pk(tb), 'ggla'], ['gate'],
                                scale=ggla[:, jv:jv + 1])
                    tm_linear(ws_it, [itb(('za', h, i))], c.KG_O, KT, hT3, 'hT', ntok, ep_za)
                for ch in range(nch):
                    sl = slice(ch * C, (ch + 1) * C)
                    pe_group([tr(PSB(4)[:C, dt * 128:(dt + 1) * 128], kdT3[:, dt, sl], identb) for dt in range(DKT)],
                             ['kdT', 'identb'], [pk(4)])
                    act(kd3[:C, ch, :], PSB(4)[:C, :DK], AF.Copy, [pk(4)], ['kd'])
                if S_in is None:
                    dve('memset', (), ['Sf'], ap=Sf, constant=0.0)
                else:
                    dma(Sf3, S_in[h * DK:(h + 1) * DK, :].rearrange("(k p) v -> p k v", p=128), ['sS%d' % h], ['Sf'])
                dve('tensor_copy', ['Sf'], ['Sb'], eng='pool', out=Sb, in_=Sf)
                NH = (DV + 511) // 512
                for ch in range(nch):
                    sl = slice(ch * C, (ch + 1) * C)
                    pe_group([mm(PS(6)[:C, :C], kpT3[:, dt, sl], qpT3[:, dt, sl], dt == 0, dt == DKT - 1) for dt in range(DKT)],
                             ['kpT', 'qpT'], [pk(6)])
                    dve('tensor_tensor', [pk(6), 'tri'], ['sc'], out=sc[:C, :C], in0=PS(6)[:C, :C], in1=tri[:C, :C], op=ALU.mult)
                    for nh in range(NH):
                        w = min(512, DV - nh * 512)
                        cs = slice(nh * 512, nh * 512 + w)
                        fns = [mm(PS(nh)[:C, :w], sc[:C, :C], vh3[:C, ch, cs], True, False)]
                        fns += [mm(PS(nh)[:C, :w], qdT3[:, dt, sl], Sb3[:, dt, cs], False, dt == DKT - 1) for dt in range(DKT)]
                        pe_group(fns, ['sc', 'vh', 'qdT', 'Sb'], [pk(nh)])
                    for nh in range(NH):
                        w = min(512, DV - nh * 512)
                        act(otok[:C, nh * 512:nh * 512 + w], PS(nh)[:C, :w], AF.Square, [pk(nh)], ['otok', 'stat'],
                            accum=stat[:C, 4 + nh:5 + nh])
                    if NH == 2:
                        dve('tensor_tensor', ['stat'], ['stat'], out=stat[:C, 4:5], in0=stat[:C, 4:5], in1=stat[:C, 5:6], op=ALU.add)
                    rstd_from_ss(stat[:C, 4:5], C, DV)
                    for nh in range(NH):
                        w = min(512, DV - nh * 512)
                        dve('tensor_scalar', [pk(nh), 'stat'], ['otok'], out=otok[:C, nh * 512:nh * 512 + w], in0=PS(nh)[:C, :w],
                            scalar1=stat[:C, 4:5], scalar2=None, op0=ALU.mult)
                    for dt in range(DKT):
                        for nh in range(NH):
                            w = min(512, DV - nh * 512)
                            cs = slice(nh * 512, nh * 512 + w)
                            bank = 2 + nh
                            pe_group([mm(PS(bank)[:, :w], kd3[:C, ch, dt * 128:(dt + 1) * 128], vh3[:C, ch, cs], True, True)],
                                     ['kd', 'vh'], [pk(bank)])
                            last = ch * C + C - 1
                            dve('scalar_tensor_tensor', [pk(bank), 'Sf', 'E3'], ['Sf'], out=Sf3[:, dt, cs], in0=Sf3[:, dt, cs],
                                scalar=E33[:, dt, last:last + 1], in1=PS(bank)[:, :w], op0=ALU.mult, op1=ALU.add)
                    dve('tensor_copy', ['Sf'], ['Sb'], eng='pool', out=Sb, in_=Sf)
                    bank = 4 + (ch % 2)
                    pe_group([tr(PSB(bank)[:, jv * 64:jv * 64 + C], otok[:C, jv * 128:(jv + 1) * 128], identb[:C, :C]) for jv in range(DVT)],
                             ['otok', 'identb'], [pk(bank)])
                    dve('tensor_tensor', [pk(bank), 'gate'], ['mergedT'], out=mT3[:, h * DVT:(h + 1) * DVT, sl],
                        in0=PSB(bank)[:, :DVT * 64].rearrange("p (j c) -> p j c", j=DVT, c=64)[:, :, :C], in1=gate3[:, :, sl], op=ALU.mult)
                for dst in S_out:
                    dma(dst[h * DK:(h + 1) * DK, :].rearrange("(k p) v -> p k v", p=128), Sf3, ['Sf'], ['sS%d' % h])

            if KSTOP < 4:
                return
            if hist == 'carry':
                if not first:
                    for g in range(8):
                        for hh in range(2):
                            dve('tensor_copy', ['kz'], ['kz'], eng='pool', out=kz4[:, g, hh, 0:128], in_=kz4[:, g, hh, NTOK:NTOK + 128])
                    dve('tensor_copy', ['Vbuf'], ['Vbuf'], eng='pool', out=Vb3[:64, 0:2, :], in_=Vb3[:64, NCHM:NCHM + 2, :])
            else:
                ckA, cvA = hist
                for part in range(2):
                    dma(cstage[:64, :], cvA[part * 64:(part + 1) * 64, :], [], ['cstage'])
                    dve('tensor_copy', ['cstage'], ['Vbuf'], out=Vb3[:64, part, :], in_=cstage[:64, :])
                for g in range(8):
                    dma(cdup[:, 0:64], ckA[:, g * 64:(g + 1) * 64], [], ['cdup'])
                    dma(cdup[:, 64:128], ckA[:, g * 64:(g + 1) * 64], [], ['cdup'])
                    dve('tensor_copy', ['cdup'], ['cdupb'], out=cdupb[:, :128], in_=cdup[:, :128])
                    pe_group([tr(PSB(7)[:, :128], cdupb[:, :128], identb)], ['cdupb', 'identb'], [pk(7)])
                    dve('tensor_copy', [pk(7)], ['kz'], out=kz4[0:64, g, 0, 0:128], in_=PSB(7)[0:64, :128])
                    dve('tensor_copy', [pk(7)], ['kz'], out=kz4[64:128, g, 1, 0:128], in_=PSB(7)[64:128, :128])
            if KSTOP < 4.08:
                return
            t0 = 0
            if kv_out is not None:
                assert kv_out[2] == ntok

            def ep_vb(bi, s_, nr, bank, w):
                act(Vb3[:nr, 2 + s_, :], PS(bank)[:nr, :512], AF.Copy, [pk(bank)], ['Vbuf'])
                if kv_out is not None:
                    dve('tensor_copy', [pk(bank)], ['cstage'], out=cstage[:nr, :], in_=PS(bank)[:nr, :512])
                    dma(kv_out[1][s_ * C:s_ * C + nr, :], cstage[:nr, :], ['cstage'], [])
            tm_linear(ws_it, [itb(('vb',))], c.KG_O, KT, hT3, 'hT', ntok, ep_vb, sub=C)
            if KSTOP < 4.2:
                return
            qT3 = v3(qT, c.QT, NTOK); sz3 = v3(sigzb, c.QT, NTOK)
            scale = 0.125

            def ep_kb(bi, s_, nr, bank, w):
                rope_key[0] = 'cstage'
                rope(bank, nr, 8, cstage)
                if kv_out is not None:
                    dma(kv_out[0][0:nr, :], cstage[:nr, :], ['cstage'], [])
                kd4 = kdup[:nr, :1024].rearrange("p (g d e) -> p g d e", g=8, d=2, e=64)
                cs3 = cstage[:nr, :512].rearrange("p (g e) -> p g e", g=8, e=64)
                dve('tensor_copy', ['cstage'], ['kdup'], out=kd4[:, :, 0, :], in_=cs3)
                dve('tensor_copy', ['cstage'], ['kdup'], eng='pool', out=kd4[:, :, 1, :], in_=cs3)
                tb = trbank()
                fm_from_tok(kdup, 'kdup', nr, 8, tb)
                pv = PSB(tb)[:, :1024].rearrange("p (g k) -> p g k", g=8, k=128)
                dve('tensor_copy', [pk(tb)], ['kz'], out=kz4[0:64, :, 0, 128:128 + nr], in_=pv[0:64, :, :nr])
                dve('tensor_copy', [pk(tb)], ['kz'], out=kz4[64:128, :, 1, 128:128 + nr], in_=pv[64:128, :, :nr])
            tm_linear(ws_it, [itb(('kb',))], c.KG_O, KT, hT3, 'hT', ntok, ep_kb)
            for g in range(8):
                def ep_qb(bi, s_, nr, bank, w):
                    rope_key[0] = 'rtok'
                    rope(bank, nr, c.G, rtok)
                    tb = trbank()
                    fm_from_tok(rtok, 'rtok', nr, c.QT, tb)
                    dve('tensor_copy', [pk(tb)], ['qT'], out=qT3[:, :, :nr],
                        in_=PSB(tb)[:, :c.QT * 128].rearrange("p (j c) -> p j c", j=c.QT, c=128)[:, :, :nr])
                tm_linear(ws_it, [itb(('qb', g))], c.KG_O, KT, hT3, 'hT', ntok, ep_qb)

                def ep_zb(bi, s_, nr, bank, w):
                    act(ztok[:nr, :w], PS(bank)[:nr, :w], AF.Sigmoid, [pk(bank)], ['ztok'])
                    tb = trbank()
                    fm_from_tok(ztok, 'ztok', nr, c.QT, tb)
                    act(sz3[:, :, :nr], PSB(tb)[:, :c.QT * 128].rearrange("p (j c) -> p j c", j=c.QT, c=128)[:, :, :nr], AF.Copy, [pk(tb)], ['sigzb'])
                tm_linear(ws_it, [itb(('zb', g))], c.KG_O, KT, hT3, 'hT', ntok, ep_zb)
                if KSTOP < 4.3:
                    continue
                for ch in range(nch):
                    sl = slice(ch * C, (ch + 1) * C)
                    if hist == 'carry':
                        parts = [(ch * 64 + 64 * i, 64, ch + i) for i in range(3)]
                        if first:
                            parts = [p_ for p_ in parts if p_[2] >= 2]
                    else:
                        parts = [(0, 64, 0), (64, 64, 1), (128, C, 2)]
                    k0 = parts[0][0]
                    nk = sum(p_[1] for p_ in parts)
                    ps4 = [PS(b_) for b_ in range(4)]
                    fns = []
                    for hh in range(c.G):
                        tq, hf = hh // 2, hh % 2
                        fns.append(mm(ps4[hh // 2][:C, (hh % 2) * 256:(hh % 2) * 256 + nk], qT3[:, tq, sl], kz4[:, g, hf, k0:k0 + nk], True, True))
                    pe_group(fns, ['qT', 'kz'], [pk(0), pk(1), pk(2), pk(3)])
                    mx = stat[:C, 8:8 + c.G]; negm = stat[:C, 16:16 + c.G]; rs = stat[:C, 24:24 + c.G]; esk = stat[:C, 32:32 + c.G]
                    for hh in range(c.G):
                        dve('tensor_reduce', [pk(hh // 2)], ['stat'], out=mx[:, hh:hh + 1], in_=ps4[hh // 2][:C, (hh % 2) * 256:(hh % 2) * 256 + nk],
                            axis=AX.X, op=ALU.max)
                    skg = sink[:C, g * c.G:(g + 1) * c.G]
                    dve('scalar_tensor_tensor', ['stat', 'sink'], ['stat'], out=mx, in0=mx, scalar=scale, in1=skg, op0=ALU.mult, op1=ALU.max)
                    dve('tensor_scalar', ['stat'], ['stat'], out=negm, in0=mx, scalar1=-1.0, scalar2=None, op0=ALU.mult)
                    pe3 = v3(pexp, 8, 256)
                    for hh in range(c.G):
                        act(pe3[:C, hh, :nk], ps4[hh // 2][:C, (hh % 2) * 256:(hh % 2) * 256 + nk], AF.Exp, [pk(hh // 2), 'stat'],
                            ['pexp', 'stat'], bias=negm[:, hh:hh + 1], scale=scale, accum=rs[:, hh:hh + 1])
                    dve('tensor_tensor', ['stat', 'sink'], ['stat'], out=esk, in0=skg, in1=mx, op=ALU.subtract)
                    act(esk, esk, AF.Exp, ['stat'], ['stat'])
                    dve('tensor_tensor', ['stat'], ['stat'], out=rs, in0=rs, in1=esk, op=ALU.add)
                    dve('reciprocal', ['stat'], ['stat'], out=rs, in_=rs)
                    if KSTOP < 4.4:
                        continue
                    pT4 = pT.rearrange("p (a h q) -> p a h q", a=3, h=8, q=64)
                    ko = 0
                    for pi, (kc0, nkp, slot) in enumerate(parts):
                        bank = 4 + (pi % 2)
                        pe_group([tr(PSB(bank)[:nkp, hh * 64:hh * 64 + C], pe3[:C, hh, ko:ko + nkp], identb[:C, :C]) for hh in range(c.G)],
                                 ['pexp', 'identb'], [pk(bank)])
                        dve('tensor_copy', [pk(bank)], ['pT'], out=pT4[:nkp, pi, :c.G, :C],
                            in_=PSB(bank)[:nkp, :c.G * 64].rearrange("p (h q) -> p h q", h=c.G, q=64)[:, :, :C])
                        ko += nkp
                    fns = []
                    for hh in range(c.G):
                        for pi, (kc0, nkp, slot) in enumerate(parts):
                            fns.append(mm(PS(6)[:C, hh * 64:(hh + 1) * 64], pT4[:nkp, pi, hh, :C], Vb3[:nkp, slot, g * 64:(g + 1) * 64],
                                          pi == 0, pi == len(parts) - 1))
                    pe_group(fns, ['pT', 'Vbuf'], [pk(6)])
                    if KSTOP < 4.5:
                        continue
                    for hh in range(c.G):
                        dve('tensor_scalar', [pk(6), 'stat'], ['ob'], out=ob[:C, hh * 64:(hh + 1) * 64], in0=PS(6)[:C, hh * 64:(hh + 1) * 64],
                            scalar1=rs[:, hh:hh + 1], scalar2=None, op0=ALU.mult)
                    pe_group([tr(PSB(7)[:, tq * 64:tq * 64 + C], ob[:C, tq * 128:(tq + 1) * 128], identb[:C, :C]) for tq in range(c.QT)],
                             ['ob', 'identb'], [pk(7)])
                    mt3 = v3(mtmp, c.QT, 64)
                    dve('tensor_tensor', [pk(7), 'sigzb'], ['mtmp'], out=mt3[:, :, :C],
                        in0=PSB(7)[:, :c.QT * 64].rearrange("p (j c) -> p j c", j=c.QT, c=64)[:, :, :C], in1=sz3[:, :, sl], op=ALU.mult)
                    dve('tensor_tensor', ['mtmp', 'mergedT'], ['mergedT'], eng='pool', out=mT3[:, g * c.QT:(g + 1) * c.QT, sl],
                        in0=mT3[:, g * c.QT:(g + 1) * c.QT, sl], in1=mt3[:, :, :C], op=ALU.add)

            if KSTOP < 5:
                return
            def ep_out(cb, s, nr, bank, w):
                cs = slice(cb * 512, (cb + 1) * 512)
                dma(cstage[:nr, :], x_src[s * 128:s * 128 + nr, cs], [], ['cstage'])
                dve('tensor_tensor', [pk(bank), 'gt1row'], ['x1_%d' % s], out=x1[s][:nr, cs], in0=PS(bank)[:nr, :], in1=gt1row[:nr, cs], op=ALU.mult)
                dve('tensor_tensor', ['x1_%d' % s, 'cstage'], ['x1_%d' % s], eng='pool', out=x1[s][:nr, cs], in0=x1[s][:nr, cs], in1=cstage[:nr, :], op=ALU.add)
            tm_linear(ws_o, [(cb, 512) for cb in range(c.NCB)], c.KG_O, KT, mT3, 'mergedT', ntok, ep_out)
            if KSTOP < 6:
                return
            for s in range(nsub):
                nr = min(128, ntok - s * 128)
                norm_to_T(x1[s], 'x1_%d' % s, s, nr, s2c, sh2c, ['s2c', 'sh2c'])
            P.barrier()
            aT3 = v3(actT, FT, NTOK)
            for fb_ in range(c.NFB):
                wf = min(512, c.DFF - fb_ * 512)

                def ep_gate(bi, s_, nr, bank, w):
                    act(gtok[:nr, :w], PS(bank)[:nr, :w], AF.Silu, [pk(bank)], ['gtok'])
                tm_linear(ws_gu, [(fb_, wf)], c.KG_O, KT, hT3, 'hT', ntok, ep_gate)

                def ep_up(bi, s_, nr, bank, w, fb_=fb_):
                    dve('tensor_tensor', [pk(bank), 'gtok'], ['ztok'], out=ztok[:nr, :w], in0=PS(bank)[:nr, :w], in1=gtok[:nr, :w], op=ALU.mult)
                    tb = trbank()
                    nt = w // 128
                    fm_from_tok(ztok, 'ztok', nr, nt, tb)
                    act(aT3[:, fb_ * 4:fb_ * 4 + nt, s_ * 128:s_ * 128 + nr],
                        PSB(tb)[:, :nt * 128].rearrange("p (j c) -> p j c", j=nt, c=128)[:, :, :nr], AF.Copy, [pk(tb)], ['actT'])
                tm_linear(ws_gu, [(c.NFB + fb_, wf)], c.KG_O, KT, hT3, 'hT', ntok, ep_up)
            def ep_dn(cb, s, nr, bank, w):
                cs = slice(cb * 512, (cb + 1) * 512)
                dve('tensor_tensor', [pk(bank), 'gt2row'], ['cstage'], out=cstage[:nr, :], in0=PS(bank)[:nr, :], in1=gt2row[:nr, cs], op=ALU.mult)
                dve('tensor_tensor', ['x1_%d' % s, 'cstage'], ['x1_%d' % s], eng='pool', out=x1[s][:nr, cs], in0=x1[s][:nr, cs], in1=cstage[:nr, :], op=ALU.add)
            tm_linear(ws_d, [(cb, 512) for cb in range(c.NCB)], c.KG_D, FT, aT3, 'actT', ntok, ep_dn)
            P.barrier()
            if KSTOP < 9:
                return
            for s in range(nsub):
                nr = min(128, ntok - s * 128)
                ss = stat[:nr, 0:1]
                act(xsb[:nr, :], x1[s][:nr, :], AF.Square, ['x1_%d' % s], ['xsb', 'stat'], accum=ss)
                rstd_from_ss(ss, nr, D)
                for cb in range(c.NCB):
                    cs = slice(cb * 512, (cb + 1) * 512)
                    dma(cstage[:nr, :], gfin[:nr, cs], [], ['cstage'])
                    dve('scalar_tensor_tensor', ['x1_%d' % s, 'stat', 'cstage'], ['x1_%d' % s], out=x1[s][:nr, cs], in0=x1[s][:nr, cs], scalar=ss,
                        in1=cstage[:nr, :], op0=ALU.mult, op1=ALU.mult)
                dma(y_dst[s * 128:s * 128 + nr, :], x1[s][:nr, :], ['x1_%d' % s], [])

        dve('memset', (), ['kz'], ap=kz, constant=0.0)
        load_mod(0)
        NT = SEQ // NTOK
        for t in range(NT):
            lastt = (t == NT - 1)
            process_tile(xp[t * NTOK:(t + 1) * NTOK, :], y_p[t * NTOK:(t + 1) * NTOK, :], NTOK, 64, t * NTOK, t == 0,
                         None if t == 0 else sS, [sg_p] if lastt else [sS], 'carry',
                         (kc_p, vc_p, 128) if lastt else None)
        for sb in range(NSB):
            load_mod(1 + sb)
            r0 = sb * c.DS
            process_tile(xs[r0:r0 + c.DS, :], y_s[r0:r0 + c.DS, :], c.DS, c.DS, SEQ, False,
                         s0[sb * 4 * DK:(sb + 1) * 4 * DK, :], [sg_s[sb * 4 * DK:(sb + 1) * 4 * DK, :]],
                         (ck[sb * 128:(sb + 1) * 128, :], cv[sb * 128:(sb + 1) * 128, :]),
                         (kc_s[r0:r0 + c.DS, :], vc_s[r0:r0 + c.DS, :], c.DS))
        P.finish()

        sems = {}
        for e in Prog.ENGS:
            sems[('e', e)] = es.enter_context(nc.semaphore(f"se_{e}"))
        for s in range(NS_DMA):
            sems[('d', s)] = es.enter_context(nc.semaphore(f"sd_{s}"))
        block = es.enter_context(nc.Block())

        @block.tensor
        def _(e):
            P.replay('pe', e, sems)

        @block.scalar
        def _(e):
            P.replay('act', e, sems)

        @block.vector
        def _(e):
            P.replay('dve', e, sems)

        @block.gpsimd
        def _(e):
            P.replay('pool', e, sems)

        @block.sync
        def _(e):
            P.replay('sp', e, sems)
    print("ops:", {e: len(P.q[e]) for e in Prog.ENGS})
    return nc


def host_inputs(cfg, inp, core):
    c = cfg
    D, KT = c.D, c.KT
    b = core
    sbs = list(range(core * c.NSB, (core + 1) * c.NSB))
    f = lambda a: np.ascontiguousarray(np.asarray(a, dtype=np.float32))
    cs = [np.asarray(inp['c_prompt'])[b]] + [np.asarray(inp['c_sample'])[s] for s in sbs]
    cT = np.zeros((128, KT, c.NBP), np.float32)
    for i, cv_ in enumerate(cs):
        cT[:, :, i] = np.asarray(cv_).reshape(KT, 128).T
    col = lambda v: f(np.asarray(v).reshape(-1, 128).T)
    d = {
        'xp': f(np.asarray(inp['x_prompt'])[b]),
        'xs': f(np.asarray(inp['x_sample'])[sbs].reshape(c.NSB * c.DS, D)),
        's0': f(np.asarray(inp['state_gla'])[0][sbs].reshape(c.NSB * 4 * c.DK, c.DV)),
        'ck': f(np.asarray(inp['cache_swa_k'])[0][sbs].reshape(c.NSB * 128, 512)),
        'cv': f(np.asarray(inp['cache_swa_v'])[0][sbs].reshape(c.NSB * 128, 512)),
        'cT': f(cT.reshape(128, KT * c.NBP)),
        'w_ada': f(np.asarray(inp['w_ada'])[0]), 'b_adaT': col(np.asarray(inp['b_ada'])[0]),
        'g1c': col(np.asarray(inp['g_norm1'])[0]), 'g2c': col(np.asarray(inp['g_norm2'])[0]),
        'gfin': f(np.broadcast_to(np.asarray(inp['g_final'])[None, :], (128, D))),
        'w_in': f(np.asarray(inp['w_in'])[0]), 'w_gk': f(np.asarray(inp['w_gk_up'])[0]),
        'bgkc': col(np.asarray(inp['b_gk'])[0]), 'gglac': col(np.asarray(inp['g_gla_out'])[0]),
        'sinkc': f(np.broadcast_to(np.asarray(inp['swa_sinks'])[0][None, :], (64, c.NQ))),
        'w_out': f(np.asarray(inp['w_out'])[0]), 'w_gu': f(np.asarray(inp['w_gate_up'])[0]), 'w_dn': f(np.asarray(inp['w_down'])[0]),
        'identf': np.eye(128, dtype=np.float32),
        'tri': np.triu(np.ones((64, 64), np.float32)),
    }
    pos = np.concatenate([np.arange(c.SEQ), c.PAST + np.arange(c.DS)]).astype(np.float32)
    inv = (np.float32(10000.0) ** (-np.arange(32, dtype=np.float32) / np.float32(32))).astype(np.float32)
    ang = (pos[None, :] * inv[:, None]).astype(np.float32)
    d['cos8'] = f(np.tile(np.cos(ang).T, (1, 8)))
    d['sin8'] = f(np.tile(np.sin(ang).T, (1, 8)))
    return d


_NC_CACHE = {}


def run(cfg, inputs):
    key = (cfg.D, cfg.DFF, cfg.SEQ, cfg.NTOK, cfg.NSB)
    if key not in _NC_CACHE:
        _NC_CACHE[key] = build(cfg)
    nc = _NC_CACHE[key]
    in_maps = [host_inputs(cfg, inputs, core) for core in range(cfg.NCORE)]
    res = run_bass_kernel_spmd(nc, in_maps, core_ids=list(range(cfg.NCORE)))
    R = res.results
    c = cfg
    nb = cfg.NCORE
    y_p = np.stack([R[i]['y_p'] for i in range(nb)])
    y_s = np.concatenate([R[i]['y_s'].reshape(c.NSB, c.DS, c.D) for i in range(nb)])
    sg_p = np.stack([R[i]['sg_p'].reshape(4, c.DK, c.DV) for i in range(nb)])[None]
    kc_p = np.stack([R[i]['kc_p'].reshape(128, 8, 64) for i in range(nb)])[None]
    vc_p = np.stack([R[i]['vc_p'].reshape(128, 8, 64) for i in range(nb)])[None]
    sg_s = np.concatenate([R[i]['sg_s'].reshape(c.NSB, 4, c.DK, c.DV) for i in range(nb)])[None]
    kc_s = np.concatenate([R[i]['kc_s'].reshape(c.NSB, c.DS, 8, 64) for i in range(nb)])[None]
    vc_s = np.concatenate([R[i]['vc_s'].reshape(c.NSB, c.DS, 8, 64) for i in range(nb)])[None]
    return tuple(np.ascontiguousarray(a, dtype=np.float32) for a in (y_p, y_s, sg_p, kc_p, vc_p, sg_s, kc_s, vc_s))


def kernel(**inputs):
    cfg = Cfg()
    return run(cfg, inputs)
```
